# Optimizing a Trainium2 kernel written in Bass

```python
import math
import jax, jax.numpy as jnp
from jax import lax
import numpy as np


D_MODEL = 4096
BATCH = 4
SEQ = 4096
DEPTH = 4

POOL_WIDTH = D_MODEL // 2
POOL_WINDOWS = (2, 4, 8, 16)
POOL_GROUP = POOL_WIDTH // len(POOL_WINDOWS)
SCONV_WIDTH = D_MODEL // 2
SCONV_K = 3
EVEN_IN = POOL_WIDTH + 3 * SCONV_WIDTH
EVEN_OUT = POOL_WIDTH + SCONV_WIDTH
HEAD_DIM = 64
N_Q_HEADS = D_MODEL // 2 // HEAD_DIM
N_KV_HEADS = N_Q_HEADS // 4
Q_PER_KV = N_Q_HEADS // N_KV_HEADS
WINDOW = 128
BLOCK = 128
Q_WIDTH = N_Q_HEADS * HEAD_DIM
KV_WIDTH = N_KV_HEADS * HEAD_DIM
SSM_WIDTH = D_MODEL // 4
SSM_GROUP_CH = 16
SSM_GROUPS = SSM_WIDTH // SSM_GROUP_CH
SSM_STATE = 64
ODD_IN = Q_WIDTH + 2 * KV_WIDTH + SSM_WIDTH
ODD_OUT = Q_WIDTH + SSM_WIDTH
REL_BUCKETS = 32
REL_MAX_DIST = 128
D_FF = ((8 * D_MODEL // 3 + 255) // 256) * 256
FFN_K = 3
EPS = 1e-6
N_EVEN = (DEPTH + 1) // 2
N_ODD = DEPTH // 2

kernel_name = 'hybrid_pool_conv_swa_s5_trunk'


def rms_norm(x, g):
    xf = x.astype(jnp.float32)
    y = xf * lax.rsqrt(jnp.mean(xf * xf, axis=-1, keepdims=True) + EPS)
    return (y * g.astype(jnp.float32)).astype(x.dtype)


def causal_dwconv(z, w):
    k, c = w.shape
    return lax.conv_general_dilated(
        z, w[:, None, :].astype(z.dtype), window_strides=(1,), padding=[(k - 1, 0)],
        dimension_numbers=('NWC', 'WIO', 'NWC'), feature_group_count=c)


def pool_mixer(a, w_grp, scale):
    L = a.shape[1]
    af = a.astype(jnp.float32)
    t = jnp.arange(1, L + 1, dtype=jnp.float32)[None, :, None]
    outs = []
    for gi, w in enumerate(POOL_WINDOWS):
        ag = af[..., gi * POOL_GROUP:(gi + 1) * POOL_GROUP]
        cs = jnp.cumsum(ag, axis=1)
        shifted = jnp.pad(cs, ((0, 0), (w, 0), (0, 0)))[:, :L]
        outs.append((cs - shifted) / jnp.minimum(t, float(w)) - ag)
    p = jnp.stack(outs, axis=2).astype(a.dtype)
    y = jnp.einsum('blgc,gcd->blgd', p, w_grp)
    return y.reshape(a.shape) * scale


def t5_causal_bucket(n):
    n = jnp.maximum(n, 0)
    max_exact = REL_BUCKETS // 2
    nf = jnp.maximum(n, 1).astype(jnp.float32)
    large = max_exact + (jnp.log(nf / max_exact) / math.log(REL_MAX_DIST / max_exact)
                         * (REL_BUCKETS - max_exact)).astype(jnp.int32)
    large = jnp.minimum(large, REL_BUCKETS - 1)
    return jnp.where(n < max_exact, n, large)


def swa_sink_attention(q, k, v, sinks, rel_bias):
    B, L = q.shape[:2]
    nb = L // BLOCK
    qb = q.reshape(B, nb, BLOCK, N_KV_HEADS, Q_PER_KV, HEAD_DIM)

    def band(t):
        tb = t.reshape(B, nb, BLOCK, N_KV_HEADS, HEAD_DIM)
        prev = jnp.pad(tb, ((0, 0), (1, 0), (0, 0), (0, 0), (0, 0)))[:, :nb]
        return jnp.concatenate([prev, tb], axis=2)

    kb, vb = band(k), band(v)
    s = jnp.einsum('bnqhgd,bnkhd->bnhgqk', qb, kb,
                   preferred_element_type=jnp.float32) * (HEAD_DIM ** -0.5)
    qi = jnp.arange(BLOCK)[:, None]
    kj = jnp.arange(2 * BLOCK)[None, :]
    dist = qi + BLOCK - kj
    bias = rel_bias[t5_causal_bucket(dist)].astype(jnp.float32)
    bias = bias.transpose(2, 0, 1).reshape(N_KV_HEADS, Q_PER_KV, BLOCK, 2 * BLOCK)
    blk = jnp.arange(nb)[:, None, None]
    valid = (dist >= 0) & (dist < WINDOW) & (blk * BLOCK + kj - BLOCK >= 0)
    s = jnp.where(valid[None, :, None, None], s + bias, -jnp.inf)
    sink = sinks.astype(jnp.float32).reshape(N_KV_HEADS, Q_PER_KV)[None, None, :, :, None, None]
    m = jnp.maximum(jnp.max(s, axis=-1, keepdims=True), sink)
    p = jnp.exp(s - m)
    p = p / (jnp.sum(p, axis=-1, keepdims=True) + jnp.exp(sink - m))
    o = jnp.einsum('bnhgqk,bnkhd->bnqhgd', p.astype(v.dtype), vb)
    return o.reshape(B, L, Q_WIDTH)


def s5_ssm_glu(u, a_re, a_im, log_step, b_re, b_im, c_re, c_im, d_skip, w_glu, b_glu):
    Bsz, L = u.shape[:2]
    f32 = jnp.float32
    uf = u.astype(f32).reshape(Bsz, L, SSM_GROUPS, SSM_GROUP_CH)
    a_re, a_im = a_re.astype(f32), a_im.astype(f32)
    b_re, b_im = b_re.astype(f32), b_im.astype(f32)
    c_re, c_im = c_re.astype(f32), c_im.astype(f32)
    dt = jnp.exp(log_step.astype(f32))[:, None]
    mag = jnp.exp(a_re * dt)
    lb_re, lb_im = mag * jnp.cos(a_im * dt), mag * jnp.sin(a_im * dt)
    den = a_re * a_re + a_im * a_im
    n_re, n_im = lb_re - 1.0, lb_im
    f_re = ((n_re * a_re + n_im * a_im) / den)[..., None]
    f_im = ((n_im * a_re - n_re * a_im) / den)[..., None]
    bb_re = f_re * b_re - f_im * b_im
    bb_im = f_re * b_im + f_im * b_re
    bu_re = jnp.einsum('blgh,gph->blgp', uf, bb_re)
    bu_im = jnp.einsum('blgh,gph->blgp', uf, bb_im)
    lam_re = jnp.broadcast_to(lb_re, bu_re.shape)
    lam_im = jnp.broadcast_to(lb_im, bu_re.shape)

    def combine(e1, e2):
        a1r, a1i, b1r, b1i = e1
        a2r, a2i, b2r, b2i = e2
        return (a2r * a1r - a2i * a1i, a2r * a1i + a2i * a1r,
                a2r * b1r - a2i * b1i + b2r, a2r * b1i + a2i * b1r + b2i)

    _, _, x_re, x_im = lax.associative_scan(combine, (lam_re, lam_im, bu_re, bu_im), axis=1)
    y = jnp.einsum('blgp,ghp->blgh', x_re, c_re) - jnp.einsum('blgp,ghp->blgh', x_im, c_im)
    y = (y + d_skip.astype(f32).reshape(SSM_GROUPS, SSM_GROUP_CH) * uf).reshape(Bsz, L, SSM_WIDTH)
    g = jax.nn.gelu(y)
    out = g * jax.nn.sigmoid(g @ w_glu.astype(f32) + b_glu.astype(f32))
    return out.astype(u.dtype)


def even_mixer(h, w_in, pool_w, pool_scale, conv_w, w_out):
    z = h @ w_in
    a = z[..., :POOL_WIDTH]
    hb = z[..., POOL_WIDTH:POOL_WIDTH + SCONV_WIDTH]
    gb = z[..., POOL_WIDTH + SCONV_WIDTH:POOL_WIDTH + 2 * SCONV_WIDTH]
    gc = z[..., POOL_WIDTH + 2 * SCONV_WIDTH:]
    ya = pool_mixer(a, pool_w, pool_scale)
    yb = gb * causal_dwconv(gc * hb, conv_w)
    return jnp.concatenate([ya, yb], axis=-1) @ w_out


def odd_mixer(h, w_in, sinks, rel_bias, a_re, a_im, log_step, b_re, b_im, c_re, c_im,
              d_skip, w_glu, b_glu, w_out):
    B, L, _ = h.shape
    z = h @ w_in
    q = z[..., :Q_WIDTH].reshape(B, L, N_Q_HEADS, HEAD_DIM)
    k = z[..., Q_WIDTH:Q_WIDTH + KV_WIDTH].reshape(B, L, N_KV_HEADS, HEAD_DIM)
    v = z[..., Q_WIDTH + KV_WIDTH:Q_WIDTH + 2 * KV_WIDTH].reshape(B, L, N_KV_HEADS, HEAD_DIM)
    u = z[..., Q_WIDTH + 2 * KV_WIDTH:]
    ya = swa_sink_attention(q, k, v, sinks, rel_bias)
    ys = s5_ssm_glu(u, a_re, a_im, log_step, b_re, b_im, c_re, c_im, d_skip, w_glu, b_glu)
    return jnp.concatenate([ya, ys], axis=-1) @ w_out


def conv_glu_ffn(h, w_up, conv_w, w_down):
    z = causal_dwconv(h @ w_up, conv_w)
    gate, up = z[..., :D_FF], z[..., D_FF:]
    return (jax.nn.silu(gate) * up) @ w_down


def setup_inputs(seed: int = 0) -> dict:
    key = jax.random.key(seed)
    ks = jax.random.split(key, 32)
    f32 = jnp.float32

    def nrm(k, shape, scale):
        return jax.random.normal(k, shape, f32) * scale

    def gain(k, shape):
        return 1.0 + 0.02 * jax.random.normal(k, shape, f32)

    a_im_base = math.pi * jnp.arange(SSM_STATE, dtype=f32)
    return {
        'x': nrm(ks[0], (BATCH, SEQ, D_MODEL), 1.0),
        'rel_bias': nrm(ks[1], (REL_BUCKETS, N_Q_HEADS), 0.5),
        'norm_mix_pre': gain(ks[2], (DEPTH, D_MODEL)),
        'norm_mix_post': gain(ks[3], (DEPTH, D_MODEL)),
        'norm_ffn_pre': gain(ks[4], (DEPTH, D_MODEL)),
        'norm_ffn_post': gain(ks[5], (DEPTH, D_MODEL)),
        'e_w_in': nrm(ks[6], (N_EVEN, D_MODEL, EVEN_IN), D_MODEL ** -0.5),
        'e_pool_w': nrm(ks[7], (N_EVEN, len(POOL_WINDOWS), POOL_GROUP, POOL_GROUP), POOL_GROUP ** -0.5),
        'e_pool_scale': gain(ks[8], (N_EVEN, POOL_WIDTH)),
        'e_conv_w': nrm(ks[9], (N_EVEN, SCONV_K, SCONV_WIDTH), SCONV_K ** -0.5),
        'e_w_out': nrm(ks[10], (N_EVEN, EVEN_OUT, D_MODEL), EVEN_OUT ** -0.5),
        'o_w_in': nrm(ks[11], (N_ODD, D_MODEL, ODD_IN), D_MODEL ** -0.5),
        'o_sinks': nrm(ks[12], (N_ODD, N_Q_HEADS), 1.0),
        'o_a_re': -0.5 + nrm(ks[13], (N_ODD, SSM_GROUPS, SSM_STATE), 0.01),
        'o_a_im': a_im_base + nrm(ks[14], (N_ODD, SSM_GROUPS, SSM_STATE), 0.01),
        'o_log_step': jax.random.uniform(ks[15], (N_ODD, SSM_GROUPS), f32,
                                         minval=math.log(1e-3), maxval=math.log(1e-1)),
        'o_b_re': nrm(ks[16], (N_ODD, SSM_GROUPS, SSM_STATE, SSM_GROUP_CH), (2 * SSM_GROUP_CH) ** -0.5),
        'o_b_im': nrm(ks[17], (N_ODD, SSM_GROUPS, SSM_STATE, SSM_GROUP_CH), (2 * SSM_GROUP_CH) ** -0.5),
        'o_c_re': nrm(ks[18], (N_ODD, SSM_GROUPS, SSM_GROUP_CH, SSM_STATE), (2 * SSM_STATE) ** -0.5),
        'o_c_im': nrm(ks[19], (N_ODD, SSM_GROUPS, SSM_GROUP_CH, SSM_STATE), (2 * SSM_STATE) ** -0.5),
        'o_d': nrm(ks[20], (N_ODD, SSM_WIDTH), 1.0),
        'o_glu_w': nrm(ks[21], (N_ODD, SSM_WIDTH, SSM_WIDTH), SSM_WIDTH ** -0.5),
        'o_glu_b': nrm(ks[22], (N_ODD, SSM_WIDTH), 0.02),
        'o_w_out': nrm(ks[23], (N_ODD, ODD_OUT, D_MODEL), ODD_OUT ** -0.5),
        'f_w_up': nrm(ks[24], (DEPTH, D_MODEL, 2 * D_FF), D_MODEL ** -0.5),
        'f_conv_w': nrm(ks[25], (DEPTH, FFN_K, 2 * D_FF), FFN_K ** -0.5),
        'f_w_down': nrm(ks[26], (DEPTH, D_FF, D_MODEL), D_FF ** -0.5),
    }


def reference(x, rel_bias, norm_mix_pre, norm_mix_post, norm_ffn_pre, norm_ffn_post,
              e_w_in, e_pool_w, e_pool_scale, e_conv_w, e_w_out,
              o_w_in, o_sinks, o_a_re, o_a_im, o_log_step, o_b_re, o_b_im, o_c_re, o_c_im,
              o_d, o_glu_w, o_glu_b, o_w_out,
              f_w_up, f_conv_w, f_w_down):
    for i in range(DEPTH):
        j = i // 2
        h = rms_norm(x, norm_mix_pre[i])
        if i % 2 == 0:
            m = even_mixer(h, e_w_in[j], e_pool_w[j], e_pool_scale[j], e_conv_w[j], e_w_out[j])
        else:
            m = odd_mixer(h, o_w_in[j], o_sinks[j], rel_bias, o_a_re[j], o_a_im[j], o_log_step[j],
                          o_b_re[j], o_b_im[j], o_c_re[j], o_c_im[j], o_d[j], o_glu_w[j],
                          o_glu_b[j], o_w_out[j])
        x = x + rms_norm(m, norm_mix_post[i])
        h = rms_norm(x, norm_ffn_pre[i])
        x = x + rms_norm(conv_glu_ffn(h, f_w_up[i], f_conv_w[i], f_w_down[i]), norm_ffn_post[i])
    return x
```

```python
import contextlib
import math
import numpy as np
import concourse.bass as bass
import concourse.mybir as mybir
from concourse.bass_utils import run_bass_kernel_spmd

F32 = mybir.dt.float32
BF16 = mybir.dt.bfloat16
AF = mybir.ActivationFunctionType
ALU = mybir.AluOpType
AX = mybir.AxisListType

SAME_ENGINE_SYNC = True


class Res:
    __slots__ = ("name", "lw", "rd", "excl")

    def __init__(self, name="", excl=False):
        self.name = name
        self.lw = None
        self.rd = {}
        self.excl = excl


class _Rec:
    def __init__(self):
        self.call = None

    def __getattr__(self, name):
        def f(*a, **k):
            assert self.call is None
            self.call = (name, a, k)
            return self
        return f


def _capture(fn):
    if fn is None:
        return None
    r = _Rec()
    fn(r)
    assert r.call is not None
    return r.call


class Prog:
    ENGS = ("pe", "act", "dve", "pool", "sp")

    def __init__(self, nc):
        self.nc = nc
        self.q = {e: [] for e in self.ENGS}
        self.cnt = {}
        self.seen = {e: {} for e in self.ENGS}
        self.semkeys = []

    def _key(self, k):
        if k not in self.cnt:
            self.cnt[k] = 0
            self.semkeys.append(k)
        return k

    def _deps(self, eng, own_key, reads, writes):
        need = {}

        def add(k, t):
            if k == own_key and (eng == "pe" or not SAME_ENGINE_SYNC):
                return
            if t > need.get(k, 0):
                need[k] = t
        for r in reads:
            if r.lw is not None:
                add(*r.lw)
        for r in writes:
            if r.lw is not None:
                add(*r.lw)
            for k, t in r.rd.items():
                add(k, t)
        waits = []
        sn = self.seen[eng]
        for k, t in need.items():
            if sn.get(k, 0) >= t:
                continue
            sn[k] = t
            waits.append((k, t))
        return waits

    def op(self, eng, fn, reads=(), writes=(), inc=True):
        if any(r.excl for r in reads):
            writes = list(writes) + [r for r in reads if r.excl]
            reads = [r for r in reads if not r.excl]
        key = self._key(eng)
        waits = self._deps(eng, key, reads, writes)
        if inc:
            self.cnt[key] += 1
            t = self.cnt[key]
        else:
            t = self.cnt[key] + 1
        for r in reads:
            if r.rd.get(key, 0) < t:
                r.rd[key] = t
        for r in writes:
            r.lw = (key, t)
            r.rd = {}
        self.q[eng].append((waits, _capture(fn), (key, 1) if inc else None))

    def dma(self, eng, semname, fn, reads=(), writes=()):
        key = self._key("dma:" + semname)
        waits = self._deps(eng, key, reads, writes)
        self.cnt[key] += 16
        t = self.cnt[key]
        for r in reads:
            if r.rd.get(key, 0) < t:
                r.rd[key] = t
        for r in writes:
            r.lw = (key, t)
            r.rd = {}
        self.q[eng].append((waits, _capture(fn), (key, 16)))

    def barrier(self, engs=("pe", "act", "dve", "pool", "sp")):
        for e in engs:
            waits = []
            for k in self.semkeys:
                if k == e or k == "sp":
                    continue
                t = self.cnt[k]
                if t > self.seen[e].get(k, 0):
                    self.seen[e][k] = t
                    waits.append((k, t))
            if waits:
                self.q[e].append((waits, None, None))

    def wait_all(self, eng, resources):
        waits = self._deps(eng, None, resources, ())
        self.q[eng].append((waits, None, None))

    def emit(self):
        nc = self.nc
        with contextlib.ExitStack() as st:
            sems = {}
            for k in self.semkeys:
                sems[k] = st.enter_context(nc.semaphore(k.replace(":", "_")))
            block = st.enter_context(nc.Block())

            def run(eng_name):
                def body(e):
                    for waits, fn, inc in self.q[eng_name]:
                        for k, t in waits:
                            e.wait_ge(sems[k], t)
                        if fn is None:
                            continue
                        ins = getattr(e, fn[0])(*fn[1], **fn[2])
                        if inc is not None:
                            ins.then_inc(sems[inc[0]], inc[1])
                return body
            if self.q["sp"]:
                block.sync(run("sp"))
            if self.q["pe"]:
                block.tensor(run("pe"))
            if self.q["act"]:
                block.scalar(run("act"))
            if self.q["dve"]:
                block.vector(run("dve"))
            if self.q["pool"]:
                block.gpsimd(run("pool"))


class Cfg:
    def __init__(self, D=4096, L=4096, DEPTH=4, NSEQ=4):
        self.D, self.L, self.DEPTH, self.NSEQ = D, L, DEPTH, NSEQ
        self.TB = 512
        self.KC = D // 128
        self.PW = D // 2
        self.PG = self.PW // 4
        self.SW = D // 2
        self.EIN = self.PW + 3 * self.SW
        self.HD = 64
        self.NQ = D // 2 // 64
        self.NKV = self.NQ // 4
        self.QW = self.NQ * 64
        self.KVW = self.NKV * 64
        self.SSW = D // 4
        self.SG = self.SSW // 16
        self.OIN = self.QW + 2 * self.KVW + self.SSW
        self.OOUT = self.QW + self.SSW
        self.DFF = ((8 * D // 3 + 255) // 256) * 256
        self.NE = (DEPTH + 1) // 2
        self.NO = DEPTH // 2
        self.NB = L // self.TB
        self.EPS = 1e-6


class Builder:
    def __init__(self, cfg, parts=("even", "odd", "ffn")):
        self.c = cfg
        self.parts = parts
        self.nc = bass.Bass("TRN2", target_bir_lowering=False)
        self.P = Prog(self.nc)
        self.st = contextlib.ExitStack()
        self.din = {}
        self._bank_i = 0
        self._units = {}

    def dram_in(self, name, shape):
        shape = list(shape)
        lim = getattr(self.c, "nl", None)
        if lim is not None and name[:2] in ("e_", "o_", "f_"):
            shape[0] = max(1, lim[name[0]])
            if lim[name[0]] == 0:
                shape = [1] * len(shape)
        self.in_shapes = getattr(self, "in_shapes", {})
        self.in_shapes[name] = shape
        ap = self.nc.dram_tensor(name, list(shape), F32, kind="ExternalInput").ap()
        self.din[name] = (ap, Res(name))
        return ap

    def sb(self, name, shape, dt):
        return self.st.enter_context(self.nc.sbuf_tensor(name, list(shape), dt))

    def bank(self):
        i = self._bank_i % 6
        self._bank_i += 1
        return self.psum[i], self.psumR[i]

    def make_units(self, key, src_name, lidx, kparts, colgroups, row0=0, pk=128):
        nu = len(kparts) * len(colgroups)
        t = self.nc.dram_tensor("wb_" + key, [nu, 128, 16 * 256], BF16, kind="Internal").ap()
        self._units[key] = dict(ap=t, R=[Res("wb_%s_%d" % (key, i)) for i in range(nu)], src=src_name,
                                lidx=lidx, kparts=kparts, colgroups=colgroups, cast=False, row0=row0, pk=pk)

    def cast_units(self, key):
        u = self._units[key]
        if u["cast"]:
            return
        u["cast"] = True
        src, srcR = self.din[u["src"]]
        w = src
        if u["lidx"] is not None:
            for ix in (u["lidx"] if isinstance(u["lidx"], tuple) else (u["lidx"],)):
                w = w[ix]
        for ci, cg in enumerate(u["colgroups"]):
            for ki, (k0, nk) in enumerate(u["kparts"]):
                ui = ci * len(u["kparts"]) + ki
                off = 0
                for (c0, ncol) in cg:
                    pk, r0 = u["pk"], u["row0"]
                    s = w[r0 + k0 * pk:r0 + (k0 + nk) * pk, c0:c0 + ncol].rearrange("(k p) c -> p k c", p=pk)
                    d = u["ap"][ui][0:pk, 0:nk * 256].rearrange("p (k c) -> p k c", c=256)[:, :, off:off + ncol]
                    self.P.dma("pool", "cast_" + key, lambda e, s=s, d=d: e.dma_start(out=d, in_=s),
                               reads=[srcR], writes=[u["R"][ui]])
                    off += ncol
        kfull = "dma:cast_" + key
        for r in u["R"]:
            r.lw = (kfull, self.P.cnt[kfull])

    def wload(self, key, ui):
        u = self._units[key]
        nkp = len(u["kparts"])
        nk = u["kparts"][ui % nkp][1]
        si = self._wslot % len(self.wslots)
        self._wslot += 1
        slot, sR = self.wslots[si], self.wslotR[si]
        pk = u["pk"]
        ncols = sum(n for _, n in u["colgroups"][ui // nkp])
        src = u["ap"][ui][0:pk, 0:nk * 256]
        dst = slot[0:pk, 0:nk * 256]
        if ncols < 256:
            src = src.rearrange("p (k c) -> p k c", c=256)[:, :, 0:ncols]
            dst = dst.rearrange("p (k c) -> p k c", c=256)[:, :, 0:ncols]
        self.P.dma("sp", "w%d" % si, lambda e, src=src, dst=dst: e.dma_start(out=dst, in_=src),
                   reads=[u["R"][ui]], writes=[sR])
        return slot[:, 0:nk * 256].rearrange("p (k c) -> p k c", c=256), sR, nk

    class WQ:
        def __init__(self, b, reqs, depth=2):
            self.b, self.reqs, self.depth = b, reqs, depth
            self.i = 0
            self.loaded = []
            for _ in range(depth):
                self._issue()

        def _issue(self):
            if self.i < len(self.reqs):
                k, ui = self.reqs[self.i]
                self.loaded.append(self.b.wload(k, ui))
                self.i += 1

        def get(self):
            r = self.loaded.pop(0)
            return r

        def done(self):
            self._issue()

    def kparts(self, nchunks):
        out, k = [], 0
        while k < nchunks:
            n = min(16, nchunks - k)
            out.append((k, n))
            k += n
        return out

    def setup(self):
        c, nc, P = self.c, self.nc, self.P
        D, L, KC, TB = c.D, c.L, c.KC, c.TB
        d = self.dram_in
        self.xT = d("xT", [D, L])
        d("consts", [128, 128 + TB])
        for n in ("norm_mix_pre", "norm_mix_post", "norm_ffn_pre", "norm_ffn_post"):
            d(n, [c.DEPTH, D])
        d("rel_bias", [32, c.NQ]); d("bmask", [128, 33 * 256])
        d("e_w_in", [c.NE, D, c.EIN]); d("e_pool_w", [c.NE, 4, c.PG, c.PG]); d("e_pool_scale", [c.NE, c.PW])
        d("e_conv_w", [c.NE, 3, c.SW]); d("e_w_out", [c.NE, D, D])
        d("o_w_in", [c.NO, D, c.OIN]); d("o_sinks", [c.NO, c.NQ])
        d("o_a_re", [c.NO, c.SG, 64]); d("o_a_im", [c.NO, c.SG, 64]); d("o_log_step", [c.NO, c.SG])
        d("o_b_re", [c.NO, c.SG, 64, 16]); d("o_b_im", [c.NO, c.SG, 64, 16])
        d("o_c_re", [c.NO, c.SG, 16, 64]); d("o_c_im", [c.NO, c.SG, 16, 64])
        d("o_d", [c.NO, c.SSW]); d("o_glu_w", [c.NO, c.SSW, c.SSW]); d("o_glu_b", [c.NO, c.SSW])
        d("o_w_out", [c.NO, c.OOUT, D])
        d("f_w_up", [c.DEPTH, D, 2 * c.DFF]); d("f_conv_w", [c.DEPTH, 3, 2 * c.DFF]); d("f_w_down", [c.DEPTH, c.DFF, D])
        self.out = nc.dram_tensor("out", [D, L], F32, kind="ExternalOutput").ap()
        self.outR = [[Res("out%d_%d" % (b, g)) for g in range(KC // 2)] for b in range(c.NB)]
        self.ybuf = nc.dram_tensor("ybuf", [D, TB], F32, kind="Internal").ap()
        self.ybufR = [Res("ybuf%d" % i) for i in range(KC)]
        kD = self.kparts(KC)
        for l in range(c.DEPTH):
            NU = c.DFF // 256
            cgs = []
            for jj in range(NU):
                cgs.append([(jj * 256, 256)])
                cgs.append([(c.DFF + jj * 256, 256)])
            self.make_units("fup%d" % l, "f_w_up", l, kD, cgs)
            self.make_units("fdn%d" % l, "f_w_down", l, self.kparts(c.DFF // 128),
                            [[(cu * 256, 256)] for cu in range(D // 256)])
        for l in range(c.NE):
            self.declare_even(l)
        for l in range(c.NO):
            self.declare_odd(l)
        self.psum = [self.st.enter_context(nc.psum_tensor("ps%d" % i, [128, 512], F32)) for i in range(8)]
        self.psumR = [Res("ps%d" % i, excl=True) for i in range(8)]
        self.cst = self.sb("cst", [128, 128 + TB], F32)
        self.cstR = Res("cst")
        self.ones = self.sb("ones", [128, 128], BF16); self.onesR = Res("ones")
        self.identb = self.sb("identb", [128, 128], BF16); self.identbR = Res("identb")
        self.gcol = self.sb("gcol", [128, 4 * c.DEPTH * KC], F32); self.gcolR = Res("gcol")
        self.stage = self.sb("stage", [128, 128], F32); self.stageR = Res("stage")
        self.hT = self.sb("hT", [128, KC, TB], BF16); self.hTR = [Res("hT%d" % k) for k in range(KC)]
        self.wslots = [self.sb("wslot%d" % i, [128, 16 * 256], BF16) for i in range(3)]
        self.wslotR = [Res("wslot%d" % i) for i in range(3)]
        self._wslot = 0
        self.xbuf = [self.sb("xbuf%d" % i, [128, 2, TB], F32) for i in range(2)]
        self.xbufR = [Res("xbuf%d" % i) for i in range(2)]
        self.ygb = [self.sb("ygb%d" % i, [128, 2, TB], F32) for i in range(2)]
        self.ygbR = [Res("ygb%d" % i) for i in range(2)]
        self.sq = [self.sb("sq%d" % i, [128, TB], BF16) for i in range(2)]
        self.sqR = [Res("sq%d" % i) for i in range(2)]
        self.ysb = [self.sb("ysb%d" % i, [128, TB], F32) for i in range(3)]
        self.ysbR = [Res("ysb%d" % i) for i in range(3)]
        self.rstd = self.sb("rstd", [128, TB], F32); self.rstdR = Res("rstd")
        self._xi = self._yi = self._sqi = self._ysi = 0
        rem = nc.sbuf_bytes_remaining
        self.ARENA = (rem - 2048) // 4
        self.arena = self.sb("arena", [128, self.ARENA], F32)
        self._ap = 0
        cap, cR = self.din["consts"]
        P.dma("sp", "cst", lambda e: e.dma_start(out=self.cst[:], in_=cap[:, :]), reads=[cR], writes=[self.cstR])
        P.op("dve", lambda e: e.memset(self.ones[:], 1.0), writes=[self.onesR])
        P.op("dve", lambda e: e.tensor_copy(out=self.identb[:], in_=self.cst[:, 0:128]), reads=[self.cstR], writes=[self.identbR])
        for gi, n in enumerate(("norm_mix_pre", "norm_mix_post", "norm_ffn_pre", "norm_ffn_post")):
            ap, r = self.din[n]
            for l in range(c.DEPTH):
                idx = gi * c.DEPTH + l
                self.colize(ap[l].rearrange("(k p) -> k p", p=128), r, KC,
                            self.gcol[:, idx * KC:(idx + 1) * KC], self.gcolR)

    def gidx(self, kind, l):
        return ({"mix_pre": 0, "mix_post": 1, "ffn_pre": 2, "ffn_post": 3}[kind] * self.c.DEPTH + l) * self.c.KC

    def colize(self, src2d, srcR, n, dst, dstR, eng="act"):
        P = self.P
        for r0 in range(0, n, 128):
            m = min(128, n - r0)
            st, stR = self.stage, self.stageR
            P.dma("sp", "stage", lambda e, r0=r0, m=m: e.dma_start(out=st[0:m, :], in_=src2d[r0:r0 + m, :]),
                  reads=[srcR], writes=[stR])
            ps, psR = self.bank()
            P.op("pe", lambda e, ps=ps, m=m: e.matmul(ps[:, 0:m], lhsT=st[0:m, :], rhs=self.cst[0:m, 0:m],
                                                      start=True, stop=True),
                 reads=[stR, self.cstR], writes=[psR])
            P.op("act", lambda e, ps=ps, m=m, r0=r0: e.activation(out=dst[:, r0:r0 + m], in_=ps[:, 0:m], func=AF.Copy),
                 reads=[psR], writes=[dstR])

    def phase_begin(self):
        self.P.barrier()
        self._ap = 0

    def carve(self, nfree, dt):
        nby = nfree * (4 if dt == F32 else 2)
        n32 = (nby + 3) // 4
        n32 = (n32 + 7) // 8 * 8
        a = self._ap
        self._ap += n32
        assert self._ap <= self.ARENA, ("arena overflow", self._ap, self.ARENA)
        v = self.arena[:, a:a + n32]
        if dt != F32:
            v = v.bitcast(dt)
        return v[:, 0:nfree]

    def _rstd_from_ss(self):
        P, c = self.P, self.c
        ss, ssR = self.psum[7], self.psumR[7]
        P.op("dve", lambda e: e.tensor_scalar(out=self.rstd[:], in0=ss[:, 0:c.TB], scalar1=1.0 / c.D, scalar2=c.EPS,
                                              op0=ALU.mult, op1=ALU.add), reads=[ssR], writes=[self.rstdR])
        P.op("act", lambda e: e.activation(out=self.rstd[:], in_=self.rstd[:], func=AF.Sqrt),
             reads=[self.rstdR], writes=[self.rstdR])
        P.op("dve", lambda e: e.reciprocal(out=self.rstd[:], in_=self.rstd[:]), reads=[self.rstdR], writes=[self.rstdR])

    def xsrc(self, s):
        if s == 0:
            return self.xT, [[self.din["xT"][1]] * (self.c.KC // 2)] * self.c.NB
        return self.out, self.outR

    def prenorm(self, s, b, g0):
        P, c = self.P, self.c
        TB, KC = c.TB, c.KC
        xs, xRs = self.xsrc(s)
        ss, ssR = self.psum[7], self.psumR[7]
        NG = KC // 2
        for ps_ in (0, 1):
            for g in range(NG):
                i = self._xi % 2
                self._xi += 1
                xb, xbR = self.xbuf[i], self.xbufR[i]
                src = xs[g * 256:(g + 1) * 256, b * TB:(b + 1) * TB].rearrange("(k p) t -> p k t", p=128)
                P.dma("sp", "xbuf%d" % i, lambda e, xb=xb, src=src: e.dma_start(out=xb[:], in_=src),
                      reads=[xRs[b][g]], writes=[xbR])
                for k in range(2):
                    kc = g * 2 + k
                    if ps_ == 0:
                        j = self._sqi % 2
                        self._sqi += 1
                        sq, sqR = self.sq[j], self.sqR[j]
                        P.op("act", lambda e, sq=sq, xb=xb, k=k: e.activation(out=sq[:], in_=xb[:, k, :], func=AF.Square),
                             reads=[xbR], writes=[sqR])
                        P.op("pe", lambda e, sq=sq, kc=kc: e.matmul(ss[:, 0:TB], lhsT=self.ones[:], rhs=sq[:],
                                                                    start=(kc == 0), stop=(kc == KC - 1)),
                             reads=[self.onesR, sqR], writes=[ssR], inc=True)
                    else:
                        P.op("dve", lambda e, xb=xb, k=k, kc=kc: e.scalar_tensor_tensor(
                            out=self.hT[:, kc, :], in0=xb[:, k, :], scalar=self.gcol[:, g0 + kc:g0 + kc + 1],
                            in1=self.rstd[:], op0=ALU.mult, op1=ALU.mult),
                            reads=[xbR, self.gcolR, self.rstdR], writes=[self.hTR[kc]])
            if ps_ == 0:
                self._rstd_from_ss()

    def post_tile(self, ps, psR, mt, nmt):
        P, c = self.P, self.c
        TB = c.TB
        ss, ssR = self.psum[7], self.psumR[7]
        i = self._ysi % 3
        self._ysi += 1
        y, yR = self.ysb[i], self.ysbR[i]
        j = self._sqi % 2
        self._sqi += 1
        sq, sqR = self.sq[j], self.sqR[j]
        P.op("act", lambda e: e.activation(out=y[:], in_=ps[:, 0:TB], func=AF.Copy), reads=[psR], writes=[yR])
        P.op("act", lambda e: e.activation(out=sq[:], in_=ps[:, 0:TB], func=AF.Square), reads=[psR], writes=[sqR])
        P.op("pe", lambda e: e.matmul(ss[:, 0:TB], lhsT=self.ones[:], rhs=sq[:], start=(mt == 0), stop=(mt == nmt - 1)),
             reads=[self.onesR, sqR], writes=[ssR], inc=True)
        P.dma("sp", "ysb%d" % i, lambda e: e.dma_start(out=self.ybuf[mt * 128:(mt + 1) * 128, :], in_=y[:]),
              reads=[yR], writes=[self.ybufR[mt]])

    def post_end(self, s, b, g0):
        P, c = self.P, self.c
        TB, KC = c.TB, c.KC
        xs, xRs = self.xsrc(s)
        self._rstd_from_ss()
        for g in range(KC // 2):
            i = self._xi % 2
            self._xi += 1
            xb, xbR = self.xbuf[i], self.xbufR[i]
            j = self._yi % 2
            self._yi += 1
            yb, ybR = self.ygb[j], self.ygbR[j]
            src = xs[g * 256:(g + 1) * 256, b * TB:(b + 1) * TB].rearrange("(k p) t -> p k t", p=128)
            ysrc = self.ybuf[g * 256:(g + 1) * 256, :].rearrange("(k p) t -> p k t", p=128)
            P.dma("sp", "xbuf%d" % i, lambda e, xb=xb, src=src: e.dma_start(out=xb[:], in_=src), reads=[xRs[b][g]], writes=[xbR])
            P.dma("sp", "ygb%d" % j, lambda e, yb=yb, ysrc=ysrc: e.dma_start(out=yb[:], in_=ysrc),
                  reads=[self.ybufR[2 * g], self.ybufR[2 * g + 1]], writes=[ybR])
            for k in range(2):
                kc = 2 * g + k
                P.op("dve", lambda e, yb=yb, k=k, kc=kc: e.scalar_tensor_tensor(
                    out=yb[:, k, :], in0=yb[:, k, :], scalar=self.gcol[:, g0 + kc:g0 + kc + 1], in1=self.rstd[:],
                    op0=ALU.mult, op1=ALU.mult), reads=[ybR, self.gcolR, self.rstdR], writes=[ybR])
                P.op("pool", lambda e, yb=yb, xb=xb, k=k: e.tensor_tensor(out=yb[:, k, :], in0=yb[:, k, :], in1=xb[:, k, :],
                                                                          op=ALU.add), reads=[ybR, xbR], writes=[ybR])
            dst = self.out[g * 256:(g + 1) * 256, b * TB:(b + 1) * TB].rearrange("(k p) t -> p k t", p=128)
            P.dma("sp", "ygb%d" % j, lambda e, yb=yb, dst=dst: e.dma_start(out=dst, in_=yb[:]),
                  reads=[ybR], writes=[self.outR[b][g]])

    def ffn_layer_setup(self, l):
        c = self.c
        NT = 2 * c.DFF // 128
        self.f_cw = self.carve(3 * NT, F32); self.f_cwR = Res("f_cw")
        ap, r = self.din["f_conv_w"]
        self.colize(ap[l].rearrange("k (j p) -> (k j) p", p=128), r, 3 * NT, self.f_cw, self.f_cwR)

    def ffn(self, s, l, b):
        P, c = self.P, self.c
        TB, KC, D = c.TB, c.KC, c.D
        NFF = c.DFF // 128
        NT = 2 * NFF
        NU = c.DFF // 256
        self.phase_begin()
        act = self.carve(NFF * TB, BF16).rearrange("p (k t) -> p k t", t=TB)
        actR = [Res("act%d" % j) for j in range(NFF)]
        tails = self.carve(NT * 2, F32).rearrange("p (j t) -> p j t", t=2)
        if b == 0:
            self.f_tailR = [Res("ftail%d" % j) for j in range(NT)]
        tailR = self.f_tailR
        zbs = [self.carve(TB + 2, F32) for _ in range(2)]
        zbR = [Res("zb%d" % i) for i in range(2)]
        accs = [self.carve(TB, F32) for _ in range(3)]
        accR = [Res("acc%d" % i) for i in range(3)]
        sgs = [self.carve(TB, F32) for _ in range(4)]
        sgR = [Res("sg%d" % i) for i in range(4)]
        self.ffn_layer_setup(l)
        cw, cwR = self.f_cw, self.f_cwR
        if b == 0:
            self.cast_units("fup%d" % l)
            self.cast_units("fdn%d" % l)
        self.prenorm(s, b, self.gidx("ffn_pre", l))
        kD = self.kparts(KC)
        kF = self.kparts(NFF)
        reqs = [("fup%d" % l, ci * len(kD) + kp) for ci in range(2 * NU) for kp in range(len(kD))]
        reqs += [("fdn%d" % l, cu * len(kF) + kp) for cu in range(D // 256) for kp in range(len(kF))]
        wq = Builder.WQ(self, reqs)
        zi = ai = 0
        for ci in range(2 * NU):
            jj, isup = divmod(ci, 2)
            banks = [self.bank(), self.bank()]
            for kp, (k0, nk) in enumerate(kD):
                w, wR, nk_ = wq.get()
                for t in (0, 1):
                    ps, psR = banks[t]
                    for k in range(nk):
                        kc = k0 + k
                        first = (kp == 0 and k == 0)
                        last = (kp == len(kD) - 1 and k == nk - 1)
                        P.op("pe", lambda e, ps=ps, w=w, k=k, t=t, kc=kc, first=first, last=last: e.matmul(
                            ps[:, 0:TB], lhsT=w[:, k, t * 128:(t + 1) * 128], rhs=self.hT[:, kc, :], start=first, stop=last),
                            reads=[wR, self.hTR[kc]], writes=[psR], inc=(k == nk - 1))
                wq.done()
            for t in (0, 1):
                ps, psR = banks[t]
                ti = (NFF if isup else 0) + 2 * jj + t
                zb, zR = zbs[zi % 2], zbR[zi % 2]
                zi += 1
                if b == 0:
                    P.op("pool", lambda e, zb=zb: e.memset(zb[:, 0:2], 0.0), writes=[zR])
                else:
                    P.op("pool", lambda e, zb=zb, ti=ti: e.tensor_copy(out=zb[:, 0:2], in_=tails[:, ti, :]),
                         reads=[tailR[ti]], writes=[zR])
                P.op("act", lambda e, zb=zb, ps=ps: e.activation(out=zb[:, 2:TB + 2], in_=ps[:, 0:TB], func=AF.Copy),
                     reads=[psR], writes=[zR])
                acc, aR = accs[ai % 3], accR[ai % 3]
                ai += 1
                P.op("dve", lambda e, acc=acc, zb=zb, ti=ti: e.tensor_scalar(
                    out=acc, in0=zb[:, 2:TB + 2], scalar1=cw[:, 2 * NT + ti:2 * NT + ti + 1], scalar2=None, op0=ALU.mult),
                    reads=[zR, cwR], writes=[aR])
                for kk in (1, 0):
                    P.op("dve", lambda e, acc=acc, zb=zb, ti=ti, kk=kk: e.scalar_tensor_tensor(
                        out=acc, in0=zb[:, kk:TB + kk], scalar=cw[:, kk * NT + ti:kk * NT + ti + 1], in1=acc,
                        op0=ALU.mult, op1=ALU.add), reads=[zR, cwR, aR], writes=[aR])
                P.op("pool", lambda e, zb=zb, ti=ti: e.tensor_copy(out=tails[:, ti, :], in_=zb[:, TB:TB + 2]),
                     reads=[zR], writes=[tailR[ti]])
                si = (jj % 2) * 2 + t
                if not isup:
                    P.op("act", lambda e, acc=acc, si=si: e.activation(out=sgs[si], in_=acc, func=AF.Silu),
                         reads=[aR], writes=[sgR[si]])
                else:
                    j = 2 * jj + t
                    P.op("dve", lambda e, acc=acc, si=si, j=j: e.tensor_tensor(out=act[:, j, :], in0=sgs[si], in1=acc, op=ALU.mult),
                         reads=[aR, sgR[si]], writes=[actR[j]])
        nmt = D // 128
        for cu in range(D // 256):
            banks = [self.bank(), self.bank()]
            for kp, (k0, nk) in enumerate(kF):
                w, wR, nk_ = wq.get()
                for t in (0, 1):
                    ps, psR = banks[t]
                    for k in range(nk):
                        kc = k0 + k
                        first = (kp == 0 and k == 0)
                        last = (kp == len(kF) - 1 and k == nk - 1)
                        P.op("pe", lambda e, ps=ps, w=w, k=k, t=t, kc=kc, first=first, last=last: e.matmul(
                            ps[:, 0:TB], lhsT=w[:, k, t * 128:(t + 1) * 128], rhs=act[:, kc, :], start=first, stop=last),
                            reads=[wR, actR[kc]], writes=[psR], inc=(k == nk - 1))
                wq.done()
            for t in (0, 1):
                self.post_tile(banks[t][0], banks[t][1], 2 * cu + t, nmt)
        self.post_end(s, b, self.gidx("ffn_post", l))

    def unit_mm(self, wq, kps, rhs_fn, rhsR_fn, nt=2, N=None, mcols=128, pk=128, banks=None, first_all=True, last_all=True):
        P, c = self.P, self.c
        N = N or c.TB
        if banks is None:
            banks = [self.bank() for _ in range(nt)]
        for kp, (k0, nk) in enumerate(kps):
            w, wR, nk_ = wq.get()
            for t in range(nt):
                ps, psR = banks[t]
                for k in range(nk):
                    kc = k0 + k
                    first = (kp == 0 and k == 0) and first_all
                    lastk = (kp == len(kps) - 1 and k == nk - 1)
                    last = lastk and last_all
                    P.op("pe", lambda e, ps=ps, w=w, k=k, t=t, kc=kc, first=first, last=last: e.matmul(
                        ps[0:mcols, 0:N], lhsT=w[0:pk, k, t * mcols:(t + 1) * mcols], rhs=rhs_fn(kc), start=first, stop=last),
                        reads=[wR, rhsR_fn(kc)], writes=[psR], inc=(k == nk - 1))
            wq.done()
        return banks

    def declare_even(self, l):
        c = self.c
        kD = self.kparts(c.KC)
        cgs = [[(u * 256, 256)] for u in range(c.PW // 256)]
        for ii in range(c.SW // 256):
            cgs.append([(c.PW + ii * 256, 256)])
            cgs.append([(c.PW + 2 * c.SW + ii * 256, 256)])
            cgs.append([(c.PW + c.SW + ii * 256, 256)])
        self.make_units("ewin%d" % l, "e_w_in", l, kD, cgs)
        for gi in range(4):
            self.make_units("epw%d_%d" % (l, gi), "e_pool_w", (l, gi), self.kparts(c.PG // 128),
                            [[(cu * 256, min(256, c.PG - cu * 256))] for cu in range((c.PG + 255) // 256)])
        self.make_units("ewout%d" % l, "e_w_out", l, kD, [[(cu * 256, 256)] for cu in range(c.D // 256)])

    def even(self, s, l, b):
        P, c = self.P, self.c
        TB, KC, D = c.TB, c.KC, c.D
        NA = c.PW // 128
        NS = c.SW // 128
        TPG = c.PG // 128
        self.phase_begin()
        ycat = self.carve(KC * TB, BF16).rearrange("p (k t) -> p k t", t=TB)
        ycR = [Res("yc%d" % k) for k in range(KC)]
        pbuf = self.carve(NA * TB, BF16).rearrange("p (k t) -> p k t", t=TB)
        pR = [Res("p%d" % k) for k in range(NA)]
        ab = [self.carve(16 + TB, F32) for _ in range(3)]
        abR = [Res("ab0"), Res("ab1"), Res("ab2")]
        ptails = self.carve(NA * 16, F32).rearrange("p (j t) -> p j t", t=16)
        ctails = self.carve(NS * 2, F32).rearrange("p (j t) -> p j t", t=2)
        if b == 0:
            self.e_ptR = [Res("ept%d" % j) for j in range(NA)]
            self.e_ctR = [Res("ect%d" % j) for j in range(NS)]
        inv = self.carve(4 * TB, F32).rearrange("p (w t) -> p w t", t=TB)
        invR = Res("inv")
        hbs = self.carve(2 * TB, F32).rearrange("p (k t) -> p k t", t=TB)
        hbR = [Res("hb0"), Res("hb1")]
        ubs = [self.carve(TB + 2, F32) for _ in range(2)]
        ubR = [Res("ub0"), Res("ub1")]
        cvs = [self.carve(TB, F32) for _ in range(2)]
        cvR = [Res("cv0"), Res("cv1")]
        cw = self.carve(3 * NS, F32); cwR = Res("ecw")
        psc = self.carve(NA, F32); pscR = Res("epsc")
        ap, r = self.din["e_conv_w"]
        self.colize(ap[l].rearrange("k (j p) -> (k j) p", p=128), r, 3 * NS, cw, cwR)
        ap, r = self.din["e_pool_scale"]
        self.colize(ap[l].rearrange("(j p) -> j p", p=128), r, NA, psc, pscR)
        if b == 0:
            self.cast_units("ewin%d" % l)
            for gi in range(4):
                self.cast_units("epw%d_%d" % (l, gi))
            self.cast_units("ewout%d" % l)
            for wi, wv in enumerate((2, 4, 8, 16)):
                P.op("dve", lambda e, wi=wi, wv=wv: e.tensor_scalar(out=inv[:, wi, :], in0=self.cst[:, 128:128 + TB], scalar1=1.0,
                                                                    scalar2=float(wv), op0=ALU.add, op1=ALU.min),
                     reads=[self.cstR], writes=[invR])
                P.op("dve", lambda e, wi=wi: e.reciprocal(out=inv[:, wi, :], in_=inv[:, wi, :]), reads=[invR], writes=[invR])
        self.prenorm(s, b, self.gidx("mix_pre", 2 * l))
        kD = self.kparts(KC)
        nkD = len(kD)
        key = "ewin%d" % l
        reqs = [(key, ci * nkD + kp) for ci in range(c.PW // 256) for kp in range(nkD)]
        for gi in range(4):
            ncu = (c.PG + 255) // 256
            reqs += [("epw%d_%d" % (l, gi), cu) for cu in range(ncu)]
        base = c.PW // 256
        reqs += [(key, (base + j) * nkD + kp) for j in range(3 * (c.SW // 256)) for kp in range(nkD)]
        reqs += [("ewout%d" % l, cu * nkD + kp) for cu in range(D // 256) for kp in range(nkD)]
        wq = Builder.WQ(self, reqs)
        hfn = lambda kc: self.hT[:, kc, :]
        hRfn = lambda kc: self.hTR[kc]
        for u in range(c.PW // 256):
            banks = self.unit_mm(wq, kD, hfn, hRfn)
            for t in (0, 1):
                at = 2 * u + t
                gi = at // TPG
                wv = (2, 4, 8, 16)[gi]
                ps, psR = banks[t]
                A, AR = ab[0], abR[0]
                B_, BR = ab[1], abR[1]
                if b == 0:
                    P.op("pool", lambda e, A=A: e.memset(A[:, 0:16], 0.0), writes=[AR])
                else:
                    P.op("pool", lambda e, A=A, at=at: e.tensor_copy(out=A[:, 0:16], in_=ptails[:, at, :]),
                         reads=[self.e_ptR[at]], writes=[AR])
                P.op("act", lambda e, A=A, ps=ps: e.activation(out=A[:, 16:16 + TB], in_=ps[:, 0:TB], func=AF.Copy),
                     reads=[psR], writes=[AR])
                P.op("pool", lambda e, A=A, at=at: e.tensor_copy(out=ptails[:, at, :], in_=A[:, TB:TB + 16]),
                     reads=[AR], writes=[self.e_ptR[at]])
                N = 16 + TB
                steps = {2: 1, 4: 2, 8: 3, 16: 4}[wv]
                cur, curR = A, AR
                for st_ in range(steps):
                    sh = 1 << st_
                    lo = (1 << (st_ + 1)) - 1
                    dst, dstR = (B_, BR) if st_ % 2 == 0 else (ab[2], abR[2])
                    P.op("dve", lambda e, cur=cur, dst=dst, sh=sh, lo=lo, N=N: e.tensor_tensor(
                        out=dst[:, lo:N], in0=cur[:, lo:N], in1=cur[:, lo - sh:N - sh], op=ALU.add),
                        reads=[curR], writes=[dstR])
                    cur, curR = dst, dstR
                if b == 0:
                    P.op("dve", lambda e, cur=cur, gi=gi: e.tensor_tensor(out=cur[:, 16:16 + TB], in0=cur[:, 16:16 + TB],
                                                                          in1=inv[:, gi, :], op=ALU.mult),
                         reads=[curR, invR], writes=[curR])
                    P.op("dve", lambda e, cur=cur, A=A, at=at: e.tensor_tensor(out=pbuf[:, at, :], in0=cur[:, 16:16 + TB],
                                                                               in1=A[:, 16:16 + TB], op=ALU.subtract),
                         reads=[curR, AR], writes=[pR[at]])
                else:
                    P.op("dve", lambda e, cur=cur, A=A, at=at, wv=wv: e.scalar_tensor_tensor(
                        out=pbuf[:, at, :], in0=cur[:, 16:16 + TB], scalar=1.0 / wv, in1=A[:, 16:16 + TB],
                        op0=ALU.mult, op1=ALU.subtract), reads=[curR, AR], writes=[pR[at]])
        for gi in range(4):
            ncu = (c.PG + 255) // 256
            for cu in range(ncu):
                nt = min(256, c.PG - cu * 256) // 128
                banks = self.unit_mm(wq, self.kparts(TPG), lambda kc, gi=gi: pbuf[:, gi * TPG + kc, :],
                                     lambda kc, gi=gi: pR[gi * TPG + kc], nt=nt)
                for t in range(nt):
                    j = gi * TPG + cu * 2 + t
                    ps, psR = banks[t]
                    P.op("act", lambda e, ps=ps, j=j: e.activation(out=ycat[:, j, :], in_=ps[:, 0:TB], func=AF.Copy,
                                                                   scale=psc[:, j:j + 1]),
                         reads=[psR, pscR], writes=[ycR[j]])
        for ii in range(c.SW // 256):
            bh = self.unit_mm(wq, kD, hfn, hRfn)
            for t in (0, 1):
                P.op("act", lambda e, t=t, ps=bh[t][0]: e.activation(out=hbs[:, t, :], in_=ps[:, 0:TB], func=AF.Copy),
                     reads=[bh[t][1]], writes=[hbR[t]])
            bc = self.unit_mm(wq, kD, hfn, hRfn)
            for t in (0, 1):
                i = 2 * ii + t
                ub, uR = ubs[t], ubR[t]
                if b == 0:
                    P.op("pool", lambda e, ub=ub: e.memset(ub[:, 0:2], 0.0), writes=[uR])
                else:
                    P.op("pool", lambda e, ub=ub, i=i: e.tensor_copy(out=ub[:, 0:2], in_=ctails[:, i, :]),
                         reads=[self.e_ctR[i]], writes=[uR])
                P.op("dve", lambda e, ub=ub, t=t, ps=bc[t][0]: e.tensor_tensor(out=ub[:, 2:TB + 2], in0=ps[:, 0:TB], in1=hbs[:, t, :],
                                                                              op=ALU.mult),
                     reads=[bc[t][1], hbR[t]], writes=[uR])
                P.op("pool", lambda e, ub=ub, i=i: e.tensor_copy(out=ctails[:, i, :], in_=ub[:, TB:TB + 2]),
                     reads=[uR], writes=[self.e_ctR[i]])
                cv, vR = cvs[t], cvR[t]
                P.op("dve", lambda e, cv=cv, ub=ub, i=i: e.tensor_scalar(out=cv, in0=ub[:, 2:TB + 2], scalar1=cw[:, 2 * NS + i:2 * NS + i + 1],
                                                                         scalar2=None, op0=ALU.mult), reads=[uR, cwR], writes=[vR])
                for kk in (1, 0):
                    P.op("dve", lambda e, cv=cv, ub=ub, i=i, kk=kk: e.scalar_tensor_tensor(
                        out=cv, in0=ub[:, kk:TB + kk], scalar=cw[:, kk * NS + i:kk * NS + i + 1], in1=cv, op0=ALU.mult, op1=ALU.add),
                        reads=[uR, cwR, vR], writes=[vR])
            bg = self.unit_mm(wq, kD, hfn, hRfn)
            for t in (0, 1):
                j = NA + 2 * ii + t
                P.op("dve", lambda e, t=t, j=j, ps=bg[t][0]: e.tensor_tensor(out=ycat[:, j, :], in0=ps[:, 0:TB], in1=cvs[t], op=ALU.mult),
                     reads=[bg[t][1], cvR[t]], writes=[ycR[j]])
        nmt = D // 128
        for cu in range(D // 256):
            banks = self.unit_mm(wq, kD, lambda kc: ycat[:, kc, :], lambda kc: ycR[kc])
            for t in (0, 1):
                self.post_tile(banks[t][0], banks[t][1], 2 * cu + t, nmt)
        self.post_end(s, b, self.gidx("mix_post", 2 * l))

    def declare_odd(self, l):
        c, nc = self.c, self.nc
        kD = self.kparts(c.KC)
        cgs = []
        self.o_nku = (c.KVW + 255) // 256
        for u in range(self.o_nku):
            cgs.append([(c.QW + u * 256, min(256, c.KVW - u * 256))])
        for u in range(self.o_nku):
            cgs.append([(c.QW + c.KVW + u * 256, min(256, c.KVW - u * 256))])
        for hk in range(c.NKV):
            cgs.append([(hk * 256, 256)])
        for u in range(c.SSW // 256):
            cgs.append([(c.QW + 2 * c.KVW + u * 256, 256)])
        self.make_units("owin%d" % l, "o_w_in", l, kD, cgs)
        self.make_units("oglu%d" % l, "o_glu_w", l, self.kparts(c.SSW // 128), [[(cu * 256, 256)] for cu in range(c.SSW // 256)])
        self.make_units("owoa%d" % l, "o_w_out", l, self.kparts(c.NQ), [[(cu * 256, 256)] for cu in range(c.D // 256)], row0=0, pk=64)
        self.make_units("owos%d" % l, "o_w_out", l, self.kparts(c.SSW // 128), [[(cu * 256, 256)] for cu in range(c.D // 256)], row0=c.QW)
        NP = c.SG // 2
        if l == 0:
            self.biasD = nc.dram_tensor("biasD", [128, c.NQ * 256], F32, kind="Internal").ap()
            self.biasDR = Res("biasD")
        self.__dict__.setdefault("lhsBD", {})[l] = nc.dram_tensor("lhsBD%d" % l, [128, NP * 256], BF16, kind="Internal").ap()
        self.__dict__.setdefault("lhsCD", {})[l] = nc.dram_tensor("lhsCD%d" % l, [128, NP * 256], BF16, kind="Internal").ap()
        self.__dict__.setdefault("lhsR", {})[l] = (Res("lhsBD%d" % l), Res("lhsCD%d" % l))
        self.__dict__.setdefault("tabD", {})[l] = nc.dram_tensor("tabD%d" % l, [128, NP * 2 * c.TB], F32, kind="Internal").ap()
        self.__dict__.setdefault("tabDR", {})[l] = Res("tabD%d" % l)

    def build_bias(self):
        P, c = self.P, self.c
        self.phase_begin()
        NQ = c.NQ
        bias = self.carve(NQ * 256, F32).rearrange("p (h k) -> p h k", k=256)
        bm = self.carve(33 * 256, F32).rearrange("p (b k) -> p b k", k=256)
        relb = self.carve(32 * NQ, F32)
        bR = [Res("bias%d" % h) for h in range(NQ)]
        bmR, rR = Res("bm"), Res("relb")
        ap, r = self.din["bmask"]
        P.dma("sp", "bm", lambda e: e.dma_start(out=bm, in_=ap.rearrange("p (b k) -> p b k", k=256)), reads=[r], writes=[bmR])
        ap2, r2 = self.din["rel_bias"]
        P.dma("sp", "relb", lambda e: e.dma_start(out=relb, in_=ap2.rearrange("b h -> (b h)").partition_broadcast(128)),
              reads=[r2], writes=[rR])
        for h in range(NQ):
            eng = "dve"
            for bk in range(32):
                in1 = bm[:, 32, :] if bk == 0 else bias[:, h, :]
                P.op(eng, lambda e, h=h, bk=bk, in1=in1: e.scalar_tensor_tensor(
                    out=bias[:, h, :], in0=bm[:, bk, :], scalar=relb[:, bk * NQ + h:bk * NQ + h + 1], in1=in1,
                    op0=ALU.mult, op1=ALU.add), reads=[bmR, rR, bR[h]], writes=[bR[h]])
        P.dma("sp", "biasD", lambda e: e.dma_start(out=self.biasD.rearrange("p (h k) -> p h k", k=256), in_=bias),
              reads=bR, writes=[self.biasDR])

    def sinred(self, dst, ti, tf, R, eng_sin="act"):
        P = self.P
        PI = math.pi
        ops = [
            lambda e: e.tensor_scalar(out=ti, in0=dst, scalar1=1.0 / (2 * PI), scalar2=None, op0=ALU.mult),
            lambda e: e.tensor_copy(out=tf, in_=ti),
            lambda e: e.scalar_tensor_tensor(out=dst, in0=tf, scalar=-2 * PI, in1=dst, op0=ALU.mult, op1=ALU.add),
            lambda e: e.tensor_scalar(out=tf, in0=dst, scalar1=-PI, scalar2=2 * PI, op0=ALU.is_lt, op1=ALU.mult),
            lambda e: e.tensor_tensor(out=dst, in0=dst, in1=tf, op=ALU.add),
            lambda e: e.tensor_scalar(out=tf, in0=dst, scalar1=PI, scalar2=-2 * PI, op0=ALU.is_gt, op1=ALU.mult),
            lambda e: e.tensor_tensor(out=dst, in0=dst, in1=tf, op=ALU.add),
            lambda e: e.tensor_scalar(out=dst, in0=dst, scalar1=-3.1415925, scalar2=3.1415925, op0=ALU.max, op1=ALU.min),
        ]
        for f in ops:
            P.op("dve", f, reads=[R], writes=[R])
        P.op("act", lambda e: e.activation(out=dst, in_=dst, func=AF.Sin), reads=[R], writes=[R])

    def odd_setup(self, l, cc, NP):
        P, c = self.P, self.c
        TB = c.TB
        PI = math.pi
        R = Res("oddc")
        cc["R"] = R
        col = cc["col"]
        A_RE, A_IM, DT, TH, MAG, C5, S5, FRE, FIM, T0, T1, T2 = range(12)

        def dv(fn, **kw):
            P.op("dve", fn, reads=[R], writes=[R])
        ap, r = self.din["o_a_re"]
        self.colize(ap[l].rearrange("(j two) p -> j (two p)", two=2), r, NP, col[:, A_RE, :], R)
        ap, r = self.din["o_a_im"]
        self.colize(ap[l].rearrange("(j two) p -> j (two p)", two=2), r, NP, col[:, A_IM, :], R)
        ap, r = self.din["o_log_step"]
        with self.nc.allow_non_contiguous_dma(reason="tiny per-layer step vector"):
            for half in (0, 1):
                src = ap[l].rearrange("(j two) -> two j", two=2)[half].partition_broadcast(64)
                P.dma("sp", "oddc", lambda e, src=src, half=half: e.dma_start(out=col[half * 64:(half + 1) * 64, DT, :], in_=src, allow_slow_non_contiguous=True),
                      reads=[r], writes=[R])
        P.op("act", lambda e: e.activation(out=col[:, DT, :], in_=col[:, DT, :], func=AF.Exp), reads=[R], writes=[R])
        dv(lambda e: e.tensor_tensor(out=col[:, TH, :], in0=col[:, A_IM, :], in1=col[:, DT, :], op=ALU.mult))
        dv(lambda e: e.tensor_tensor(out=col[:, T0, :], in0=col[:, A_RE, :], in1=col[:, DT, :], op=ALU.mult))
        P.op("act", lambda e: e.activation(out=col[:, MAG, :], in_=col[:, T0, :], func=AF.Exp), reads=[R], writes=[R])

        def sincos(dst_s, dst_c, mult):
            for dst, sh in ((dst_s, 0.0), (dst_c, 0.5 * PI)):
                dv(lambda e, dst=dst, sh=sh: e.tensor_scalar(out=dst, in0=col[:, TH, :], scalar1=float(mult), scalar2=float(sh),
                                                             op0=ALU.mult, op1=ALU.add))
                self.sinred(dst, cc["ci32"], cc["cf32"], R)
        sincos(col[:, T1, :], col[:, T2, :], 1.0)
        dv(lambda e: e.tensor_tensor(out=col[:, T1, :], in0=col[:, T1, :], in1=col[:, MAG, :], op=ALU.mult))
        dv(lambda e: e.tensor_tensor(out=col[:, T2, :], in0=col[:, T2, :], in1=col[:, MAG, :], op=ALU.mult))
        dv(lambda e: e.tensor_scalar(out=col[:, T2, :], in0=col[:, T2, :], scalar1=-1.0, scalar2=None, op0=ALU.add))
        dv(lambda e: e.tensor_tensor(out=col[:, T0, :], in0=col[:, A_RE, :], in1=col[:, A_RE, :], op=ALU.mult))
        dv(lambda e: e.tensor_tensor(out=col[:, FRE, :], in0=col[:, A_IM, :], in1=col[:, A_IM, :], op=ALU.mult))
        dv(lambda e: e.tensor_tensor(out=col[:, T0, :], in0=col[:, T0, :], in1=col[:, FRE, :], op=ALU.add))
        dv(lambda e: e.reciprocal(out=col[:, T0, :], in_=col[:, T0, :]))
        dv(lambda e: e.tensor_tensor(out=col[:, FRE, :], in0=col[:, T2, :], in1=col[:, A_RE, :], op=ALU.mult))
        dv(lambda e: e.tensor_tensor(out=col[:, FIM, :], in0=col[:, T1, :], in1=col[:, A_IM, :], op=ALU.mult))
        dv(lambda e: e.tensor_tensor(out=col[:, FRE, :], in0=col[:, FRE, :], in1=col[:, FIM, :], op=ALU.add))
        dv(lambda e: e.tensor_tensor(out=col[:, FIM, :], in0=col[:, T1, :], in1=col[:, A_RE, :], op=ALU.mult))
        dv(lambda e: e.tensor_tensor(out=col[:, T1, :], in0=col[:, T2, :], in1=col[:, A_IM, :], op=ALU.mult))
        dv(lambda e: e.tensor_tensor(out=col[:, FIM, :], in0=col[:, FIM, :], in1=col[:, T1, :], op=ALU.subtract))
        dv(lambda e: e.tensor_tensor(out=col[:, FRE, :], in0=col[:, FRE, :], in1=col[:, T0, :], op=ALU.mult))
        dv(lambda e: e.tensor_tensor(out=col[:, FIM, :], in0=col[:, FIM, :], in1=col[:, T0, :], op=ALU.mult))
        sincos(col[:, S5, :], col[:, C5, :], float(TB))
        bst = cc["bst"]
        bb = cc["bb"]
        tmp = cc["btmp"]
        with self.nc.allow_non_contiguous_dma(reason="per-layer SSM B (64B runs)"):
            for ri, nm in enumerate(("o_b_re", "o_b_im")):
                ap, r = self.din[nm]
                for half in (0, 1):
                    src = ap[l].rearrange("(j two) p h -> two p j h", two=2)[half]
                    P.dma("sp", "oddc", lambda e, src=src, ri=ri, half=half: e.dma_start(out=bst[half * 64:(half + 1) * 64, ri, :, :], in_=src),
                          reads=[r], writes=[R])
        fre_b = col[:, FRE, :].unsqueeze(2).to_broadcast([128, NP, 16])
        fim_b = col[:, FIM, :].unsqueeze(2).to_broadcast([128, NP, 16])
        dv(lambda e: e.tensor_tensor(out=bb[:, 0, :, :], in0=bst[:, 0, :, :], in1=fre_b, op=ALU.mult))
        dv(lambda e: e.tensor_tensor(out=tmp, in0=bst[:, 1, :, :], in1=fim_b, op=ALU.mult))
        dv(lambda e: e.tensor_tensor(out=bb[:, 0, :, :], in0=bb[:, 0, :, :], in1=tmp, op=ALU.subtract))
        dv(lambda e: e.tensor_tensor(out=bb[:, 1, :, :], in0=bst[:, 1, :, :], in1=fre_b, op=ALU.mult))
        dv(lambda e: e.tensor_tensor(out=tmp, in0=bst[:, 0, :, :], in1=fim_b, op=ALU.mult))
        dv(lambda e: e.tensor_tensor(out=bb[:, 1, :, :], in0=bb[:, 1, :, :], in1=tmp, op=ALU.add))
        X = cc["X"]
        XR = Res("X")
        lst = cc["lst"]
        lstR = Res("lst")
        lst4 = lst.rearrange("p (j r m) -> p j r m", r=2, m=128)
        for j in range(NP):
            for ri in (0, 1):
                P.op("dve", lambda e: e.memset(X, 0.0), writes=[XR])
                for half in (0, 1):
                    g8 = (2 * j + half) % 8
                    P.op("dve", lambda e, j=j, ri=ri, half=half, g8=g8: e.tensor_copy(
                        out=X[half * 64:(half + 1) * 64, g8 * 16:(g8 + 1) * 16], in_=bb[half * 64:(half + 1) * 64, ri, j, :]),
                        reads=[R], writes=[XR])
                ps, psR = self.bank()
                psb = ps[:].bitcast(BF16)
                P.op("pe", lambda e, psb=psb: e.transpose(out=psb[:, 0:128], in_=X, identity=self.identb[:]),
                     reads=[XR, self.identbR], writes=[psR])
                P.op("act", lambda e, psb=psb, j=j, ri=ri: e.activation(out=lst4[:, j, ri, :], in_=psb[:, 0:128], func=AF.Copy),
                     reads=[psR], writes=[lstR])
        BR_, CR_ = self.lhsR[l]
        P.dma("sp", "lhsst", lambda e: e.dma_start(out=self.lhsBD[l], in_=lst), reads=[lstR], writes=[BR_])
        ct = cc["ct"]
        ctb = cc["ctb"]
        ctR = Res("ct")
        P.op("dve", lambda e: e.memset(lst, 0.0), reads=[lstR], writes=[lstR])
        for ri, nm in enumerate(("o_c_re", "o_c_im")):
            ap, r = self.din[nm]
            rows = ap[l].rearrange("g h p -> (g h) p")
            for i in range(c.SSW // 128):
                for dup in (0, 1):
                    P.dma("sp", "ct", lambda e, i=i, dup=dup, rows=rows: e.dma_start(out=ct[:, dup * 64:(dup + 1) * 64], in_=rows[i * 128:(i + 1) * 128, :]),
                          reads=[r], writes=[ctR])
                P.op("act", lambda e, ri=ri: e.activation(out=ctb, in_=ct, func=AF.Copy, scale=(1.0 if ri == 0 else -1.0)),
                     reads=[ctR], writes=[ctR])
                ps, psR = self.bank()
                psb = ps[:].bitcast(BF16)
                P.op("pe", lambda e, psb=psb: e.transpose(out=psb[:, 0:128], in_=ctb, identity=self.identb[:]),
                     reads=[ctR, self.identbR], writes=[psR])
                for jj in range(4):
                    j = 4 * i + jj
                    for half in (0, 1):
                        g8 = (2 * j + half) % 8
                        P.op("act", lambda e, psb=psb, j=j, ri=ri, half=half, g8=g8: e.activation(
                            out=lst4[half * 64:(half + 1) * 64, j, ri, g8 * 16:(g8 + 1) * 16],
                            in_=psb[half * 64:(half + 1) * 64, g8 * 16:(g8 + 1) * 16], func=AF.Copy),
                            reads=[psR], writes=[lstR])
        P.dma("sp", "lhsst", lambda e: e.dma_start(out=self.lhsCD[l], in_=lst), reads=[lstR], writes=[CR_])
        tb2 = cc["tb2"]
        tbR = [Res("tb2_0"), Res("tb2_1")]
        for j in range(NP):
            jb = j % 2
            for ri, sh in ((0, 0.0), (1, 0.5 * PI)):
                P.op("dve", lambda e, j=j, jb=jb, ri=ri, sh=sh: e.tensor_scalar(out=tb2[:, jb, ri, :], in0=self.cst[:, 128:128 + TB],
                                                                               scalar1=col[:, TH, j:j + 1], scalar2=float(sh), op0=ALU.mult, op1=ALU.add),
                     reads=[self.cstR, R, tbR[jb]], writes=[tbR[jb]])
                self.sinred(tb2[:, jb, ri, :], cc["ti32"], cc["tf32"], tbR[jb])
            P.dma("sp", "tabst%d" % jb, lambda e, j=j, jb=jb: e.dma_start(
                out=self.tabD[l][:, j * 2 * TB:(j + 1) * 2 * TB].rearrange("p (r t) -> p r t", t=TB), in_=tb2[:, jb, :, :]),
                reads=[tbR[jb]], writes=[self.tabDR[l]])
        ap, r = self.din["o_d"]
        self.colize(ap[l].rearrange("(j p) -> j p", p=128), r, c.SSW // 128, cc["dcol"], R)
        ap, r = self.din["o_glu_b"]
        self.colize(ap[l].rearrange("(j p) -> j p", p=128), r, c.SSW // 128, cc["gbcol"], R)
        ap, r = self.din["o_sinks"]
        P.dma("sp", "oddc", lambda e: e.dma_start(out=cc["sinkb"], in_=ap[l].partition_broadcast(128)), reads=[r], writes=[R])

    def odd(self, s, l, b):
        P, c = self.P, self.c
        TB, KC, D = c.TB, c.KC, c.D
        NQ, NKV, KVW, SSW = c.NQ, c.NKV, c.KVW, c.SSW
        UT = SSW // 128
        NP = c.SG // 2
        PI = math.pi
        if not getattr(self, "_bias_built", False):
            self._bias_built = True
            self.build_bias()
        import os as _os
        STOP = float(_os.environ.get("ODD_STOP", "99"))
        if STOP <= 1:
            return
        self.phase_begin()
        yatt = self.carve(NQ * TB, BF16).rearrange("p (h t) -> p h t", t=TB)
        yaR = [Res("yatt%d" % h) for h in range(NQ)]
        yssm = self.carve(UT * TB, BF16).rearrange("p (k t) -> p k t", t=TB)
        ysR = [Res("yssm%d" % k) for k in range(UT)]
        kcar = self.carve(NKV * 128, BF16).rearrange("p (h t) -> p h t", t=128)
        vcar = self.carve(KVW, BF16)
        scar = self.carve(NP * 2, F32).rearrange("p (j r) -> p j r", r=2)
        cc = dict(col=self.carve(12 * NP, F32).rearrange("p (q j) -> p q j", j=NP),
                  dcol=self.carve(UT, F32), gbcol=self.carve(UT, F32), sinkb=self.carve(NQ, F32))
        if b == 0:
            self.o_carR = dict(k=Res("kcar"), v=Res("vcar"), s=[Res("scar%d" % j) for j in range(NP)])
        carR = self.o_carR
        mark = self._ap
        if b == 0:
            for kname in ("owin%d", "oglu%d", "owoa%d", "owos%d"):
                self.cast_units(kname % l)
            cc.update(bst=self.carve(2 * NP * 16, F32).rearrange("p (r j h) -> p r j h", r=2, h=16),
                      bb=self.carve(2 * NP * 16, F32).rearrange("p (r j h) -> p r j h", r=2, h=16),
                      btmp=self.carve(NP * 16, F32).rearrange("p (j h) -> p j h", h=16),
                      X=self.carve(128, BF16), lst=self.carve(NP * 256, BF16), ct=self.carve(128, F32), ctb=self.carve(128, BF16),
                      ci32=self.carve(NP, F32).bitcast(mybir.dt.int32), cf32=self.carve(NP, F32),
                      ti32=self.carve(TB, F32).bitcast(mybir.dt.int32), tf32=self.carve(TB, F32),
                      tb2=self.carve(4 * TB, F32).rearrange("p (a r t) -> p a r t", a=2, r=2))
            self.odd_setup(l, cc, NP)
            self.o_ccR = cc["R"]
            self.P.barrier()
            self._ap = mark
        if STOP <= 2:
            return
        ccR = self.o_ccR
        col = cc["col"]
        A_RE, A_IM, DT, TH, MAG, C5, S5, FRE, FIM, T0, T1, T2 = range(12)
        bias = self.carve(NQ * 256, F32).rearrange("p (h k) -> p h k", k=256)
        biasR = Res("bias")
        kT = self.carve(NKV * (128 + TB), BF16).rearrange("p (h t) -> p h t", t=128 + TB)
        kTR = [Res("kT%d" % h) for h in range(NKV)]
        vtm = self.carve(5 * KVW, BF16).rearrange("p (s v) -> p s v", v=KVW)
        vR = [Res("vtm%d" % i) for i in range(5)]
        qT = self.carve(4 * TB, BF16).rearrange("p (i t) -> p i t", t=TB)
        qR = [Res("q%d" % i) for i in range(4)]
        sc = self.carve(4 * 256, F32).rearrange("p (i k) -> p i k", k=256); scR = Res("sc")
        pf = self.carve(4 * 256, F32).rearrange("p (i k) -> p i k", k=256); pfR = Res("pf")
        pb = self.carve(4 * 256, BF16).rearrange("p (i k) -> p i k", k=256); pbR = Res("pb")
        pTs = self.carve(2 * 512, BF16).rearrange("p (kb n) -> p kb n", n=512); pTR = Res("pTs")
        stt = self.carve(8 * 4, F32).rearrange("p (a i) -> p a i", i=4); stR = Res("stt")
        P.dma("sp", "bias", lambda e: e.dma_start(out=bias, in_=self.biasD.rearrange("p (h k) -> p h k", k=256)),
              reads=[self.biasDR], writes=[biasR])
        self.prenorm(s, b, self.gidx("mix_pre", 2 * l + 1))
        kD = self.kparts(KC)
        nkD = len(kD)
        key = "owin%d" % l
        nku = self.o_nku
        reqs = [(key, ci * nkD + kp) for ci in range(2 * nku + NKV + SSW // 256) for kp in range(nkD)]
        nkU, nkQ = len(self.kparts(UT)), len(self.kparts(NQ))
        reqs += [("oglu%d" % l, cu * nkU + kp) for cu in range(SSW // 256) for kp in range(nkU)]
        for cu in range(D // 256):
            reqs += [("owoa%d" % l, cu * nkQ + kp) for kp in range(nkQ)]
            reqs += [("owos%d" % l, cu * nkU + kp) for kp in range(nkU)]
        wq = Builder.WQ(self, reqs)
        hfn = lambda kc: self.hT[:, kc, :]
        hRfn = lambda kc: self.hTR[kc]
        if b == 0:
            P.op("pool", lambda e: e.memset(vtm[:, 0, :], 0.0), writes=[vR[0]])
            for h in range(NKV):
                P.op("pool", lambda e, h=h: e.memset(kT[:, h, 0:128], 0.0), writes=[kTR[h]])
        else:
            P.op("pool", lambda e: e.tensor_copy(out=vtm[:, 0, :], in_=vcar), reads=[carR["v"]], writes=[vR[0]])
            for h in range(NKV):
                P.op("pool", lambda e, h=h: e.tensor_copy(out=kT[0:64, h, 0:128], in_=kcar[0:64, h, :]), reads=[carR["k"]], writes=[kTR[h]])
        for u in range(nku):
            ncol = min(256, KVW - u * 256)
            nt = ncol // 64
            banks = self.unit_mm(wq, kD, hfn, hRfn, nt=nt, mcols=64)
            for t in range(nt):
                h = u * 4 + t
                P.op("act", lambda e, h=h, ps=banks[t][0]: e.activation(out=kT[0:64, h, 128:128 + TB], in_=ps[0:64, 0:TB], func=AF.Copy),
                     reads=[banks[t][1]], writes=[kTR[h]])
        for h in range(NKV):
            P.op("pool", lambda e, h=h: e.tensor_copy(out=kcar[0:64, h, :], in_=kT[0:64, h, TB:TB + 128]), reads=[kTR[h]], writes=[carR["k"]])
        for u in range(nku):
            ncol = min(256, KVW - u * 256)
            banks = [self.bank() for _ in range(4)]
            for kp, (k0, nk) in enumerate(kD):
                w, wR, _ = wq.get()
                for sb in range(4):
                    ps, psR = banks[sb]
                    for k in range(nk):
                        kc = k0 + k
                        first = (kp == 0 and k == 0)
                        last = (kp == nkD - 1 and k == nk - 1)
                        P.op("pe", lambda e, ps=ps, w=w, k=k, kc=kc, sb=sb, first=first, last=last, ncol=ncol: e.matmul(
                            ps[:, 0:ncol], lhsT=self.hT[:, kc, sb * 128:(sb + 1) * 128], rhs=w[:, k, 0:ncol], start=first, stop=last),
                            reads=[wR, self.hTR[kc]], writes=[psR], inc=(k == nk - 1))
                wq.done()
            for sb in range(4):
                P.op("act", lambda e, sb=sb, u=u, ncol=ncol, ps=banks[sb][0]: e.activation(
                    out=vtm[:, 1 + sb, u * 256:u * 256 + ncol], in_=ps[:, 0:ncol], func=AF.Copy),
                    reads=[banks[sb][1]], writes=[vR[1 + sb]])
        P.op("pool", lambda e: e.tensor_copy(out=vcar, in_=vtm[:, 4, :]), reads=[vR[4]], writes=[carR["v"]])
        RMAX, MM, NEGM, RSUM, ES, DEN, RDEN, TMP = range(8)
        if STOP <= 3:
            return
        for hk in range(NKV):
            banks = self.unit_mm(wq, kD, hfn, hRfn, nt=4, mcols=64)
            for i in range(4):
                P.op("act", lambda e, i=i, ps=banks[i][0]: e.activation(out=qT[0:64, i, :], in_=ps[0:64, 0:TB], func=AF.Copy),
                     reads=[banks[i][1]], writes=[qR[i]])
            for sb in range(4):
                full = not (b == 0 and sb == 0)
                W = 256 if full else 128
                koff = 0 if full else 128
                sbanks = [self.bank(), self.bank()]
                for i in range(4):
                    ps, psR = sbanks[i // 2]
                    o0 = (i % 2) * 256
                    P.op("pe", lambda e, ps=ps, i=i, o0=o0, W=W, koff=koff, sb=sb, hk=hk: e.matmul(
                        ps[:, o0:o0 + W], lhsT=qT[0:64, i, sb * 128:(sb + 1) * 128],
                        rhs=kT[0:64, hk, sb * 128 + koff:sb * 128 + 256], start=True, stop=True),
                        reads=[qR[i], kTR[hk]], writes=[psR])
                for i in range(4):
                    ps, psR = sbanks[i // 2]
                    o0 = (i % 2) * 256
                    P.op("dve", lambda e, ps=ps, i=i, o0=o0, W=W, koff=koff, hk=hk: e.scalar_tensor_tensor(
                        out=sc[:, i, 0:W], in0=ps[:, o0:o0 + W], scalar=0.125, in1=bias[:, 4 * hk + i, koff:256],
                        op0=ALU.mult, op1=ALU.add), reads=[psR, biasR], writes=[scR])
                P.op("dve", lambda e, W=W: e.tensor_reduce(out=stt[:, RMAX, :], in_=sc[:, :, 0:W], axis=AX.X, op=ALU.max),
                     reads=[scR], writes=[stR])
                P.op("dve", lambda e, hk=hk: e.tensor_tensor(out=stt[:, MM, :], in0=stt[:, RMAX, :], in1=cc["sinkb"][:, 4 * hk:4 * hk + 4], op=ALU.max),
                     reads=[stR, ccR], writes=[stR])
                P.op("dve", lambda e: e.tensor_scalar(out=stt[:, NEGM, :], in0=stt[:, MM, :], scalar1=-1.0, scalar2=None, op0=ALU.mult),
                     reads=[stR], writes=[stR])
                P.op("dve", lambda e: e.memset(stt[:, RSUM, :], 0.0), reads=[stR], writes=[stR])
                for i in range(4):
                    P.op("act", lambda e, i=i, W=W: e.activation(out=pf[:, i, 0:W], in_=sc[:, i, 0:W], func=AF.Exp,
                                                                 bias=stt[:, NEGM, i:i + 1], accum_out=stt[:, RSUM, i:i + 1]),
                         reads=[scR, stR], writes=[pfR, stR])
                P.op("dve", lambda e, hk=hk: e.tensor_tensor(out=stt[:, TMP, :], in0=cc["sinkb"][:, 4 * hk:4 * hk + 4], in1=stt[:, NEGM, :], op=ALU.add),
                     reads=[stR, ccR], writes=[stR])
                P.op("act", lambda e: e.activation(out=stt[:, ES, :], in_=stt[:, TMP, :], func=AF.Exp), reads=[stR], writes=[stR])
                P.op("dve", lambda e: e.tensor_tensor(out=stt[:, DEN, :], in0=stt[:, RSUM, :], in1=stt[:, ES, :], op=ALU.add), reads=[stR], writes=[stR])
                P.op("dve", lambda e: e.reciprocal(out=stt[:, RDEN, :], in_=stt[:, DEN, :]), reads=[stR], writes=[stR])
                for i in range(4):
                    P.op("pool", lambda e, i=i, W=W: e.tensor_scalar(out=pb[:, i, 0:W], in0=pf[:, i, 0:W], scalar1=stt[:, RDEN, i:i + 1],
                                                                     scalar2=None, op0=ALU.mult), reads=[pfR, stR], writes=[pbR])
                tb_, tbR = self.bank()
                tbb = tb_[:].bitcast(BF16).rearrange("p (kb n) -> p kb n", n=512)
                kbs = (0, 1) if full else (1,)
                for i in range(4):
                    for kb in kbs:
                        c0 = (kb if full else 0) * 128
                        P.op("pe", lambda e, i=i, kb=kb, c0=c0: e.transpose(out=tbb[:, kb, i * 128:(i + 1) * 128], in_=pb[:, i, c0:c0 + 128],
                                                                            identity=self.identb[:]),
                             reads=[pbR, self.identbR], writes=[tbR])
                for kb in kbs:
                    P.op("act", lambda e, kb=kb: e.activation(out=pTs[:, kb, :], in_=tbb[:, kb, :], func=AF.Copy), reads=[tbR], writes=[pTR])
                pv, pvR = self.bank()
                for n_, kb in enumerate(kbs):
                    P.op("pe", lambda e, kb=kb, n_=n_, sb=sb, hk=hk: e.matmul(
                        pv[0:64, 0:512], lhsT=vtm[:, sb + kb, hk * 64:(hk + 1) * 64], rhs=pTs[:, kb, :],
                        start=(n_ == 0), stop=(n_ == len(kbs) - 1)), reads=[vR[sb + kb], pTR], writes=[pvR], inc=(n_ == len(kbs) - 1))
                P.op("act", lambda e, sb=sb, hk=hk: e.activation(
                    out=yatt[0:64, 4 * hk:4 * hk + 4, sb * 128:(sb + 1) * 128],
                    in_=pv[0:64, 0:512].rearrange("p (i q) -> p i q", q=128), func=AF.Copy),
                    reads=[pvR], writes=[yaR[4 * hk + i] for i in range(4)])
        if STOP <= 4:
            return
        self.P.barrier()
        self._ap = mark
        if STOP <= 4.05:
            return
        u32 = self.carve(UT * TB, F32).rearrange("p (k t) -> p k t", t=TB)
        uR = [Res("u32_%d" % k) for k in range(UT)]
        ub = self.carve(UT * TB, BF16).rearrange("p (k t) -> p k t", t=TB)
        ubR = [Res("ub_%d" % k) for k in range(UT)]
        gb = self.carve(UT * TB, BF16).rearrange("p (k t) -> p k t", t=TB)
        gR = [Res("gb_%d" % k) for k in range(UT)]
        lB = self.carve(4 * 256, BF16).rearrange("p (j r m) -> p j r m", r=2, m=128); lBR = Res("lB")
        lC = self.carve(4 * 256, BF16).rearrange("p (j r m) -> p j r m", r=2, m=128); lCR = Res("lC")
        tabt = self.carve(4 * TB, F32).rearrange("p (a r t) -> p a r t", a=2, r=2)
        tabs2 = [(tabt[:, a, 0, :], tabt[:, a, 1, :]) for a in range(2)]
        tabR2 = [Res("tab0"), Res("tab1")]
        rin = [self.carve(TB, F32) for _ in range(4)]
        rinR = Res("rin")
        xs_ = [self.carve(TB, F32) for _ in range(2)]
        xsR = Res("xs")
        po = [self.carve(TB, F32) for _ in range(2)]
        poR = Res("po")
        xb16 = [[self.carve(TB, BF16) for _ in range(2)] for _ in range(2)]
        xbR = [Res("xb0"), Res("xb1")]
        init = self.carve(4, F32); initR = Res("init")
        yt = self.carve(TB, F32); ytR = Res("yt")
        t3 = self.carve(TB, F32); t3R = Res("t3")
        for u in range(SSW // 256):
            banks = self.unit_mm(wq, kD, hfn, hRfn)
            for t in (0, 1):
                i = 2 * u + t
                P.op("act", lambda e, i=i, ps=banks[t][0]: e.activation(out=u32[:, i, :], in_=ps[:, 0:TB], func=AF.Copy),
                     reads=[banks[t][1]], writes=[uR[i]])
                if STOP <= 4.07:
                    continue
                P.op("dve", lambda e, i=i: e.tensor_copy(out=ub[:, i, :], in_=u32[:, i, :]), reads=[uR[i]], writes=[ubR[i]])
        if STOP <= 4.1:
            return
        BR_, CR_ = self.lhsR[l]
        ybank, ybR = self.psum[6], self.psumR[6]
        pcount = 0
        for i in range(UT):
            P.dma("sp", "lB", lambda e, i=i: e.dma_start(out=lB, in_=self.lhsBD[l][:, i * 1024:(i + 1) * 1024].rearrange("p (j r m) -> p j r m", r=2, m=128)),
                  reads=[BR_], writes=[lBR])
            P.dma("sp", "lC", lambda e, i=i: e.dma_start(out=lC, in_=self.lhsCD[l][:, i * 1024:(i + 1) * 1024].rearrange("p (j r m) -> p j r m", r=2, m=128)),
                  reads=[CR_], writes=[lCR])
            for jj in range(4):
                j = 4 * i + jj
                tb_i = pcount % 2
                tabs, tabc, tabR = tabs2[tb_i][0], tabs2[tb_i][1], tabR2[tb_i]
                P.dma("sp", "tab%d" % tb_i, lambda e, j=j, tb_i=tb_i: e.dma_start(
                    out=tabt[:, tb_i, :, :], in_=self.tabD[l][:, j * 2 * TB:(j + 1) * 2 * TB].rearrange("p (r t) -> p r t", t=TB)),
                    reads=[self.tabDR[l]], writes=[tabR])
                bre, breR = self.bank()
                bim, bimR = self.bank()
                for ri, (ps, psR) in enumerate(((bre, breR), (bim, bimR))):
                    P.op("pe", lambda e, ps=ps, jj=jj, ri=ri, i=i: e.matmul(ps[:, 0:TB], lhsT=lB[:, jj, ri, :], rhs=ub[:, i, :], start=True, stop=True),
                         reads=[lBR, ubR[i]], writes=[psR])
                brp, bip, t1, t2 = rin
                P.op("dve", lambda e: e.tensor_tensor(out=brp, in0=bre[:, 0:TB], in1=tabc, op=ALU.mult), reads=[breR, tabR, rinR], writes=[rinR])
                P.op("dve", lambda e: e.tensor_tensor(out=t1, in0=bim[:, 0:TB], in1=tabs, op=ALU.mult), reads=[bimR, tabR, rinR], writes=[rinR])
                P.op("dve", lambda e: e.tensor_tensor(out=brp, in0=brp, in1=t1, op=ALU.add), reads=[rinR], writes=[rinR])
                P.op("dve", lambda e: e.tensor_tensor(out=bip, in0=bim[:, 0:TB], in1=tabc, op=ALU.mult), reads=[bimR, tabR, rinR], writes=[rinR])
                P.op("dve", lambda e: e.tensor_tensor(out=t2, in0=bre[:, 0:TB], in1=tabs, op=ALU.mult), reads=[breR, tabR, rinR], writes=[rinR])
                P.op("dve", lambda e: e.tensor_tensor(out=bip, in0=bip, in1=t2, op=ALU.subtract), reads=[rinR], writes=[rinR])
                if STOP <= 4.2:
                    return
                xr_, xi_ = xs_
                if b == 0:
                    ini_r, ini_i = 0.0, 0.0
                    rd = []
                else:
                    ini_r, ini_i = scar[:, j, 0:1], scar[:, j, 1:2]
                    rd = [carR["s"][j]]
                magb = col[:, MAG, j:j + 1].to_broadcast([128, TB])
                P.op("dve", lambda e, ini_r=ini_r, magb=magb: e.tensor_tensor_scan(out=xr_, data0=magb, data1=brp, initial=ini_r, op0=ALU.mult, op1=ALU.add),
                     reads=[rinR, ccR, xsR] + rd, writes=[xsR])
                P.op("dve", lambda e, ini_i=ini_i, magb=magb: e.tensor_tensor_scan(out=xi_, data0=magb, data1=bip, initial=ini_i, op0=ALU.mult, op1=ALU.add),
                     reads=[rinR, ccR, xsR] + rd, writes=[xsR])
                if STOP <= 4.3:
                    return
                P.op("dve", lambda e, j=j: e.tensor_scalar(out=init[:, 0:1], in0=xi_[:, TB - 1:TB], scalar1=col[:, S5, j:j + 1], scalar2=None, op0=ALU.mult),
                     reads=[xsR, ccR, initR], writes=[initR])
                P.op("dve", lambda e, j=j: e.scalar_tensor_tensor(out=scar[:, j, 0:1], in0=xr_[:, TB - 1:TB], scalar=col[:, C5, j:j + 1], in1=init[:, 0:1],
                                                                  op0=ALU.mult, op1=ALU.subtract), reads=[xsR, ccR, initR], writes=[carR["s"][j]])
                P.op("dve", lambda e, j=j: e.tensor_scalar(out=init[:, 1:2], in0=xi_[:, TB - 1:TB], scalar1=col[:, C5, j:j + 1], scalar2=None, op0=ALU.mult),
                     reads=[xsR, ccR, initR], writes=[initR])
                P.op("dve", lambda e, j=j: e.scalar_tensor_tensor(out=scar[:, j, 1:2], in0=xr_[:, TB - 1:TB], scalar=col[:, S5, j:j + 1], in1=init[:, 1:2],
                                                                  op0=ALU.mult, op1=ALU.add), reads=[xsR, ccR, initR], writes=[carR["s"][j]])
                if STOP <= 4.4:
                    return
                xrb, xib = xb16[pcount % 2]
                xbRr = xbR[pcount % 2]
                pcount += 1
                p0, p1 = po
                P.op("pool", lambda e: e.tensor_tensor(out=p0, in0=xr_, in1=tabc, op=ALU.mult), reads=[xsR, tabR, poR], writes=[poR])
                P.op("pool", lambda e: e.tensor_tensor(out=p1, in0=xi_, in1=tabs, op=ALU.mult), reads=[xsR, tabR, poR], writes=[poR])
                P.op("pool", lambda e, xrb=xrb: e.tensor_tensor(out=xrb, in0=p0, in1=p1, op=ALU.subtract), reads=[poR, xbRr], writes=[xbRr])
                P.op("pool", lambda e: e.tensor_tensor(out=p0, in0=xr_, in1=tabs, op=ALU.mult), reads=[xsR, tabR, poR], writes=[poR])
                P.op("pool", lambda e: e.tensor_tensor(out=p1, in0=xi_, in1=tabc, op=ALU.mult), reads=[xsR, tabR, poR], writes=[poR])
                P.op("pool", lambda e, xib=xib: e.tensor_tensor(out=xib, in0=p0, in1=p1, op=ALU.add), reads=[poR, xbRr], writes=[xbRr])
                if STOP <= 4.5:
                    return
                for ri, xx in enumerate((xrb, xib)):
                    P.op("pe", lambda e, jj=jj, ri=ri, xx=xx: e.matmul(ybank[:, 0:TB], lhsT=lC[:, jj, ri, :], rhs=xx,
                                                                      start=(jj == 0 and ri == 0), stop=(jj == 3 and ri == 1)),
                         reads=[lCR, xbRr], writes=[ybR], inc=True)
            if STOP <= 4.6:
                return
            P.op("dve", lambda e, i=i: e.scalar_tensor_tensor(out=yt, in0=u32[:, i, :], scalar=cc["dcol"][:, i:i + 1], in1=ybank[:, 0:TB],
                                                              op0=ALU.mult, op1=ALU.add), reads=[uR[i], ccR, ybR, ytR], writes=[ytR])
            P.op("dve", lambda e: e.tensor_tensor(out=t3, in0=yt, in1=yt, op=ALU.mult), reads=[ytR, t3R], writes=[t3R])
            P.op("dve", lambda e: e.tensor_scalar(out=t3, in0=t3, scalar1=0.044715, scalar2=1.0, op0=ALU.mult, op1=ALU.add), reads=[t3R], writes=[t3R])
            P.op("dve", lambda e: e.tensor_tensor(out=t3, in0=t3, in1=yt, op=ALU.mult), reads=[t3R, ytR], writes=[t3R])
            P.op("act", lambda e: e.activation(out=t3, in_=t3, func=AF.Sigmoid, scale=1.5957691216057308), reads=[t3R], writes=[t3R])
            P.op("dve", lambda e, i=i: e.tensor_tensor(out=u32[:, i, :], in0=yt, in1=t3, op=ALU.mult), reads=[ytR, t3R], writes=[uR[i]])
            P.op("act", lambda e, i=i: e.activation(out=gb[:, i, :], in_=u32[:, i, :], func=AF.Copy), reads=[uR[i]], writes=[gR[i]])
        if STOP <= 5:
            return
        kU = self.kparts(UT)
        for cu in range(SSW // 256):
            banks = self.unit_mm(wq, kU, lambda kc: gb[:, kc, :], lambda kc: gR[kc])
            for t in (0, 1):
                i = 2 * cu + t
                P.op("act", lambda e, i=i, ps=banks[t][0]: e.activation(out=t3, in_=ps[:, 0:TB], func=AF.Sigmoid, bias=cc["gbcol"][:, i:i + 1]),
                     reads=[banks[t][1], ccR, t3R], writes=[t3R])
                P.op("dve", lambda e, i=i: e.tensor_tensor(out=yssm[:, i, :], in0=u32[:, i, :], in1=t3, op=ALU.mult), reads=[uR[i], t3R], writes=[ysR[i]])
        nmt = D // 128
        kQ = self.kparts(NQ)
        for cu in range(D // 256):
            banks = self.unit_mm(wq, kQ, lambda h: yatt[0:64, h, :], lambda h: yaR[h], pk=64, last_all=False)
            banks = self.unit_mm(wq, kU, lambda kc: yssm[:, kc, :], lambda kc: ysR[kc], banks=banks, first_all=False)
            for t in (0, 1):
                self.post_tile(banks[t][0], banks[t][1], 2 * cu + t, nmt)
        self.post_end(s, b, self.gidx("mix_post", 2 * l + 1))

    def build(self, sublayers):
        c, P = self.c, self.P
        self.setup()
        for s, (kind, l) in enumerate(sublayers):
            for b in range(c.NB):
                if kind == "ffn":
                    self.ffn(s, l, b)
                elif kind == "even":
                    self.even(s, l, b)
                else:
                    self.odd(s, l, b)
        P.wait_all("sp", [r for rs in self.outR for r in rs])
        P.emit()
        return self.nc


def make_consts(TB=512):
    cst = np.zeros((128, 128 + TB), np.float32)
    cst[:, :128] = np.eye(128, dtype=np.float32)
    cst[:, 128:] = np.arange(TB, dtype=np.float32)[None, :]
    return cst


def make_bmask():
    qi = np.arange(128)[:, None]
    kj = np.arange(256)[None, :]
    dist = qi + 128 - kj
    valid = (dist >= 0) & (dist < 128)
    n = np.maximum(dist, 0)
    nf = np.maximum(n, 1).astype(np.float32)
    large = 16 + (np.log(nf / np.float32(16)) / np.float32(math.log(128 / 16)) * np.float32(16)).astype(np.int32)
    large = np.minimum(large, 31)
    bucket = np.where(n < 16, n, large)
    m = np.zeros((128, 33, 256), np.float32)
    for bk in range(32):
        m[:, bk, :] = ((bucket == bk) & valid).astype(np.float32)
    m[:, 32, :] = np.where(valid, 0.0, -30000.0)
    return m.reshape(128, 33 * 256)


def make_in_map(cfg, inputs, seq):
    im = {"xT": np.ascontiguousarray(np.asarray(inputs["x"][seq], np.float32).T),
          "consts": make_consts(cfg.TB), "bmask": make_bmask()}
    for k, v in inputs.items():
        if k != "x":
            im[k] = np.ascontiguousarray(np.asarray(v, np.float32))
    return im


_NC_CACHE = {}


def kernel(**inputs):
    x = np.asarray(inputs["x"], np.float32)
    B, L, D = x.shape
    cfg = Cfg(D=D, L=L, DEPTH=4, NSEQ=B)
    subl = []
    for i in range(cfg.DEPTH):
        subl += [("even" if i % 2 == 0 else "odd", i // 2), ("ffn", i)]
    key = (D, L, B)
    if key not in _NC_CACHE:
        _NC_CACHE[key] = Builder(cfg).build(subl)
    nc = _NC_CACHE[key]
    shared = {"consts": make_consts(cfg.TB), "bmask": make_bmask()}
    for k, v in inputs.items():
        if k != "x":
            shared[k] = np.ascontiguousarray(np.asarray(v, np.float32))
    in_maps = []
    for s in range(B):
        im = dict(shared)
        im["xT"] = np.ascontiguousarray(x[s].T)
        in_maps.append(im)
    res = run_bass_kernel_spmd(nc, in_maps, core_ids=list(range(B)))
    out = np.stack([np.ascontiguousarray(np.asarray(r["out"], np.float32).T) for r in res.results], axis=0)
    return out.astype(np.float32)
```
